# Optimizing a Trainium2 kernel written in Bass

```python
import math
import jax, jax.numpy as jnp
from jax import lax
import numpy as np

D_MODEL = 1024
BATCH = 2
SEQ = 8192
DEPTH = 2

HEAD_DIM = 64
H_MLA = 8
H_MOBA = 8
MLA_WIDTH = H_MLA * HEAD_DIM
MOBA_WIDTH = H_MOBA * HEAD_DIM
D_MIX = MLA_WIDTH + MOBA_WIDTH
Q_LORA = 256
KV_LORA = 128
NOPE_DIM = 64
ROPE_DIM = 32
V_DIM = HEAD_DIM
QK_DIM = NOPE_DIM + ROPE_DIM
ROPE_THETA = 10000.0
MOBA_BLOCK = 256
MOBA_TOPK = 3
Q_BLOCK = 128
MOBA_Q_CHUNK = 64
N_BUCKETS = 32
REL_MAX_DIST = 4096
EPS = 1e-6

IN_SPLITS = (Q_LORA, KV_LORA, ROPE_DIM, MLA_WIDTH, MOBA_WIDTH, MOBA_WIDTH, MOBA_WIDTH, MOBA_WIDTH)
IN_OFFSETS = (Q_LORA,
              Q_LORA + KV_LORA,
              Q_LORA + KV_LORA + ROPE_DIM,
              Q_LORA + KV_LORA + ROPE_DIM + MLA_WIDTH,
              Q_LORA + KV_LORA + ROPE_DIM + MLA_WIDTH + MOBA_WIDTH,
              Q_LORA + KV_LORA + ROPE_DIM + MLA_WIDTH + 2 * MOBA_WIDTH,
              Q_LORA + KV_LORA + ROPE_DIM + MLA_WIDTH + 3 * MOBA_WIDTH)
D_IN = Q_LORA + KV_LORA + ROPE_DIM + MLA_WIDTH + 4 * MOBA_WIDTH

kernel_name = "hybrid_mla_moba_adaln_block"


def rms_norm(x, g):
    xf = x.astype(jnp.float32)
    y = xf * lax.rsqrt(jnp.mean(xf * xf, axis=-1, keepdims=True) + EPS)
    return (y * g.astype(jnp.float32)).astype(x.dtype)


def apply_rope(x, positions):
    half = ROPE_DIM // 2
    inv_freq = ROPE_THETA ** (-jnp.arange(0, half, dtype=jnp.float32) / half)
    ang = positions.astype(jnp.float32)[..., None] * inv_freq
    cos = jnp.cos(ang)[:, :, None, :]
    sin = jnp.sin(ang)[:, :, None, :]
    xf = x.astype(jnp.float32)
    x1, x2 = xf[..., :half], xf[..., half:]
    out = jnp.concatenate([x1 * cos - x2 * sin, x1 * sin + x2 * cos], axis=-1)
    return out.astype(x.dtype)


def rel_bucket(dist):
    n = jnp.maximum(dist, 0)
    max_exact = N_BUCKETS // 2
    nf = jnp.maximum(n, 1).astype(jnp.float32)
    large = max_exact + (jnp.log(nf / max_exact) / math.log(REL_MAX_DIST / max_exact)
                         * (N_BUCKETS - max_exact)).astype(jnp.int32)
    large = jnp.minimum(large, N_BUCKETS - 1)
    return jnp.where(n < max_exact, n, large)


def mla_attention(c_q, c_kv, k_rope, positions, q_norm_g, w_uq, kv_norm_g, w_ukv, q_g, k_g):
    B, S, _ = c_q.shape
    q = (rms_norm(c_q, q_norm_g) @ w_uq).reshape(B, S, H_MLA, QK_DIM)
    kv = (rms_norm(c_kv, kv_norm_g) @ w_ukv).reshape(B, S, H_MLA, NOPE_DIM + V_DIM)
    k_nope, v = kv[..., :NOPE_DIM], kv[..., NOPE_DIM:]
    k = jnp.concatenate(
        [k_nope, jnp.broadcast_to(k_rope[:, :, None, :], (B, S, H_MLA, ROPE_DIM))], axis=-1)
    q = rms_norm(q, q_g)
    k = rms_norm(k, k_g)
    q = jnp.concatenate([q[..., :NOPE_DIM], apply_rope(q[..., NOPE_DIM:], positions)], axis=-1)
    k = jnp.concatenate([k[..., :NOPE_DIM], apply_rope(k[..., NOPE_DIM:], positions)], axis=-1)
    n_qb = S // Q_BLOCK
    qb = q.reshape(B, n_qb, Q_BLOCK, H_MLA, QK_DIM).transpose(1, 0, 2, 3, 4)
    kpos = jnp.arange(S)
    scale = QK_DIM ** -0.5

    def block(args):
        q_blk, i = args
        s = jnp.einsum('bqhd,bkhd->bhqk', q_blk, k,
                       preferred_element_type=jnp.float32) * scale
        qpos = i * Q_BLOCK + jnp.arange(Q_BLOCK)
        s = jnp.where(kpos[None, :] <= qpos[:, None], s, -jnp.inf)
        p = jax.nn.softmax(s, axis=-1)
        return jnp.einsum('bhqk,bkhd->bqhd', p.astype(v.dtype), v)

    o = lax.map(block, (qb, jnp.arange(n_qb)))
    return o.transpose(1, 0, 2, 3, 4).reshape(B, S, MLA_WIDTH)


def moba_attention(q, k, v, q_g, k_g, rel_bias):
    B, S, _ = q.shape
    q = rms_norm(q.reshape(B, S, H_MOBA, HEAD_DIM), q_g).transpose(0, 2, 1, 3)
    k = rms_norm(k.reshape(B, S, H_MOBA, HEAD_DIM), k_g).transpose(0, 2, 1, 3)
    v = v.reshape(B, S, H_MOBA, HEAD_DIM).transpose(0, 2, 1, 3)
    nb = -(-S // MOBA_BLOCK)
    pad = nb * MOBA_BLOCK - S
    kp = jnp.pad(k, ((0, 0), (0, 0), (0, pad), (0, 0))).reshape(B, H_MOBA, nb, MOBA_BLOCK, HEAD_DIM)
    vp = jnp.pad(v, ((0, 0), (0, 0), (0, pad), (0, 0))).reshape(B, H_MOBA, nb, MOBA_BLOCK, HEAD_DIM)
    blk_ids = jnp.arange(nb)
    counts = jnp.minimum(S - blk_ids * MOBA_BLOCK, MOBA_BLOCK).astype(jnp.float32)
    k_mean = kp.astype(jnp.float32).sum(axis=3) / counts[None, None, :, None]
    topk = min(MOBA_TOPK, nb)
    n_qc = S // MOBA_Q_CHUNK
    qc = q.reshape(B, H_MOBA, n_qc, MOBA_Q_CHUNK, HEAD_DIM).transpose(2, 0, 1, 3, 4)
    scale = HEAD_DIM ** -0.5
    bias_t = rel_bias.astype(jnp.float32).T
    head_idx = jnp.arange(H_MOBA)[None, :, None, None, None]
    gather_blocks = jax.vmap(jax.vmap(lambda blocks, idx: blocks[idx]))

    def chunk(args):
        q_c, i = args
        q0 = i * MOBA_Q_CHUNK
        own = q0 // MOBA_BLOCK
        qpos = q0 + jnp.arange(MOBA_Q_CHUNK)
        g = jnp.einsum('bhqd,bhnd->bhqn', q_c.astype(jnp.float32), k_mean)
        g = jnp.where(blk_ids < own, g, -jnp.inf)
        _, sel = lax.top_k(g, topk)
        valid = (jnp.arange(topk) < own)[:, None]
        k_sel = gather_blocks(kp, sel)
        v_sel = gather_blocks(vp, sel)
        s_sel = jnp.einsum('bhqd,bhqtkd->bhqtk', q_c, k_sel,
                           preferred_element_type=jnp.float32) * scale
        kpos_sel = sel[..., None] * MOBA_BLOCK + jnp.arange(MOBA_BLOCK)
        bias_sel = bias_t[head_idx, rel_bucket(qpos[:, None, None] - kpos_sel)]
        s_sel = jnp.where(valid, s_sel + bias_sel, -jnp.inf)
        k_own = lax.dynamic_slice_in_dim(kp, own, 1, axis=2)[:, :, 0]
        v_own = lax.dynamic_slice_in_dim(vp, own, 1, axis=2)[:, :, 0]
        s_own = jnp.einsum('bhqd,bhkd->bhqk', q_c, k_own,
                           preferred_element_type=jnp.float32) * scale
        kpos_own = own * MOBA_BLOCK + jnp.arange(MOBA_BLOCK)
        dist_own = qpos[:, None] - kpos_own[None, :]
        bias_own = bias_t[:, rel_bucket(dist_own)]
        s_own = jnp.where(dist_own >= 0, s_own + bias_own[None], -jnp.inf)
        s_all = jnp.concatenate(
            [s_sel.reshape(B, H_MOBA, MOBA_Q_CHUNK, topk * MOBA_BLOCK), s_own], axis=-1)
        p = jax.nn.softmax(s_all, axis=-1).astype(v.dtype)
        p_sel = p[..., :topk * MOBA_BLOCK].reshape(B, H_MOBA, MOBA_Q_CHUNK, topk, MOBA_BLOCK)
        p_own = p[..., topk * MOBA_BLOCK:]
        return (jnp.einsum('bhqtk,bhqtkd->bhqd', p_sel, v_sel)
                + jnp.einsum('bhqk,bhkd->bhqd', p_own, v_own))

    o = lax.map(chunk, (qc, jnp.arange(n_qc)))
    o = o.transpose(1, 2, 0, 3, 4).reshape(B, H_MOBA, S, HEAD_DIM)
    return o.transpose(0, 2, 1, 3).reshape(B, S, MOBA_WIDTH)


def hybrid_layer(x, c, positions, norm_g, w_ada, b_ada, w_in, mla_q_norm_g, mla_w_uq,
                 mla_kv_norm_g, mla_w_ukv, mla_q_g, mla_k_g, moba_q_g, moba_k_g, w_out, rel_bias):
    mod = jax.nn.silu(c) @ w_ada + b_ada
    shift, scale, gate = jnp.split(mod[:, None, :], 3, axis=-1)
    h = rms_norm(x, norm_g) * (1 + scale) + shift
    z = h @ w_in
    c_q, c_kv, k_rope, g_mla, q_b, k_b, v_b, g_moba = jnp.split(z, list(IN_OFFSETS), axis=-1)
    o_mla = mla_attention(c_q, c_kv, k_rope, positions, mla_q_norm_g, mla_w_uq,
                          mla_kv_norm_g, mla_w_ukv, mla_q_g, mla_k_g)
    o_moba = moba_attention(q_b, k_b, v_b, moba_q_g, moba_k_g, rel_bias)
    y = jnp.concatenate([jax.nn.silu(g_mla) * o_mla, jax.nn.silu(g_moba) * o_moba], axis=-1) @ w_out
    return x + gate * y


def setup_inputs(seed: int = 0) -> dict:
    key = jax.random.key(seed)
    ks = jax.random.split(key, 20)
    f32 = jnp.float32
    nrm = lambda k, shape, s: jax.random.normal(k, shape, f32) * s
    return {
        "x": nrm(ks[0], (BATCH, SEQ, D_MODEL), 1.0),
        "c": nrm(ks[1], (BATCH, D_MODEL), 1.0),
        "positions": jnp.broadcast_to(jnp.arange(SEQ, dtype=jnp.int32), (BATCH, SEQ)),
        "norm_g": 1.0 + nrm(ks[2], (DEPTH, D_MODEL), 0.05),
        "w_ada": nrm(ks[3], (DEPTH, D_MODEL, 3 * D_MODEL), 0.5 * D_MODEL ** -0.5),
        "b_ada": nrm(ks[4], (DEPTH, 3 * D_MODEL), 0.01),
        "w_in": nrm(ks[5], (DEPTH, D_MODEL, D_IN), D_MODEL ** -0.5),
        "mla_q_norm_g": 1.0 + nrm(ks[6], (DEPTH, Q_LORA), 0.05),
        "mla_w_uq": nrm(ks[7], (DEPTH, Q_LORA, H_MLA * QK_DIM), Q_LORA ** -0.5),
        "mla_kv_norm_g": 1.0 + nrm(ks[8], (DEPTH, KV_LORA), 0.05),
        "mla_w_ukv": nrm(ks[9], (DEPTH, KV_LORA, H_MLA * (NOPE_DIM + V_DIM)), KV_LORA ** -0.5),
        "mla_q_g": 1.0 + nrm(ks[10], (DEPTH, QK_DIM), 0.05),
        "mla_k_g": 1.0 + nrm(ks[11], (DEPTH, QK_DIM), 0.05),
        "moba_q_g": 1.0 + nrm(ks[12], (DEPTH, HEAD_DIM), 0.05),
        "moba_k_g": 1.0 + nrm(ks[13], (DEPTH, HEAD_DIM), 0.05),
        "w_out": nrm(ks[14], (DEPTH, D_MIX, D_MODEL), D_MIX ** -0.5),
        "rel_bias": nrm(ks[15], (N_BUCKETS, H_MOBA), 0.5),
    }


def reference(x, c, positions, norm_g, w_ada, b_ada, w_in, mla_q_norm_g, mla_w_uq,
              mla_kv_norm_g, mla_w_ukv, mla_q_g, mla_k_g, moba_q_g, moba_k_g, w_out, rel_bias):
    for l in range(DEPTH):
        x = hybrid_layer(x, c, positions, norm_g[l], w_ada[l], b_ada[l], w_in[l],
                         mla_q_norm_g[l], mla_w_uq[l], mla_kv_norm_g[l], mla_w_ukv[l],
                         mla_q_g[l], mla_k_g[l], moba_q_g[l], moba_k_g[l], w_out[l], rel_bias)
    return x
```

```python
import math
from contextlib import ExitStack

import numpy as np
import ml_dtypes

import concourse.bass as bass
import concourse.mybir as mybir
from concourse.bass_utils import run_bass_kernel_spmd

F32 = mybir.dt.float32
BF16 = mybir.dt.bfloat16
I32 = mybir.dt.int32
AF = mybir.ActivationFunctionType
ALU = mybir.AluOpType
AX = mybir.AxisListType

D = 1024
SEQ = 8192
NB = 2
T = 512
NT = SEQ // T
EPS = 1e-6
BIG = 30000.0
NCOL = 1088
LR = 3456
WZ = LR + 128
FAR = 3072
TWO_PI = 2.0 * math.pi
CW1 = 6.28125
CW2 = TWO_PI - CW1

ENGS = ("pe", "act", "dve", "pool", "sp")
NDMA_SEM = 8


class Sched:
    def __init__(self, nc, same_engine_raw=True):
        self.nc = nc
        self.ops = {e: [] for e in ENGS}
        self.bufs = {}
        self.same_engine_raw = same_engine_raw
        self.psum_keys = set()

    def add(self, eng, fn, reads=(), writes=(), dma=False):
        idx = len(self.ops[eng])
        deps = set()
        writes = list(writes) + [k for k in reads if k in self.psum_keys and k not in writes]

        def need(o_eng, o_dma, kind):
            if o_dma or dma:
                return True
            if o_eng != eng:
                return True
            return kind == "raw" and self.same_engine_raw and eng != "pe"

        for k in reads:
            st = self.bufs.setdefault(k, {"w": [], "r": []})
            for (we, wi, wd) in st["w"]:
                if need(we, wd, "raw"):
                    deps.add((we, wi))
        for k in writes:
            st = self.bufs.setdefault(k, {"w": [], "r": []})
            for (we, wi, wd) in st["w"]:
                if need(we, wd, "waw"):
                    deps.add((we, wi))
            for (re_, ri, rd) in st["r"]:
                if need(re_, rd, "war"):
                    deps.add((re_, ri))
        for k in reads:
            st = self.bufs[k]
            if not dma:
                st["r"] = [x for x in st["r"] if not (x[0] == eng and not x[2])]
            st["r"].append((eng, idx, dma))
        for k in writes:
            st = self.bufs[k]
            st["w"] = [(eng, idx, dma)]
            st["r"] = []
        deps.discard((eng, idx))
        self.ops[eng].append({"fn": fn, "deps": deps, "dma": dma, "sig": False})
        return (eng, idx)

    def emit(self, stack):
        nc = self.nc
        ops = self.ops
        for e in ENGS:
            for op in ops[e]:
                for (de, di) in op["deps"]:
                    if not ops[de][di]["dma"]:
                        ops[de][di]["sig"] = True
        esem = {e: stack.enter_context(nc.semaphore("s_" + e)) for e in ENGS}
        dsem = {e: [stack.enter_context(nc.semaphore("d_%s%d" % (e, i))) for i in range(NDMA_SEM)]
                for e in ENGS if any(o["dma"] for o in ops[e])}
        for e in ENGS:
            cnt = 0
            nd = 0
            for op in ops[e]:
                if op["dma"]:
                    op["ev"] = (dsem[e][nd % NDMA_SEM], 16 * (nd // NDMA_SEM + 1))
                    op["prev"] = (dsem[e][nd % NDMA_SEM], 16 * (nd // NDMA_SEM)) if nd >= NDMA_SEM else None
                    nd += 1
                elif op["sig"]:
                    cnt += 1
                    op["ev"] = (esem[e], cnt)
        self.counts = {e: sum(1 for o in ops[e] if o["sig"]) for e in ENGS}

        def run(e, engine):
            waited = {}

            def wait(ev):
                sem, val = ev
                if waited.get(id(sem), 0) >= val:
                    return
                engine.wait_ge(sem, val)
                waited[id(sem)] = val

            for op in ops[e]:
                for (de, di) in sorted(op["deps"]):
                    wait(ops[de][di]["ev"])
                if op["dma"] and op["prev"] is not None:
                    wait(op["prev"])
                ins = op["fn"](engine)
                if ins is None:
                    assert not op["dma"] and not op["sig"]
                    continue
                if op["dma"]:
                    ins.then_inc(op["ev"][0], 16)
                elif op["sig"]:
                    ins.then_inc(op["ev"][0], 1)

        block = stack.enter_context(nc.Block())

        @block.tensor
        def _(eng):
            run("pe", eng)

        @block.scalar
        def _(eng):
            run("act", eng)

        @block.vector
        def _(eng):
            run("dve", eng)

        @block.gpsimd
        def _(eng):
            run("pool", eng)

        @block.sync
        def _(eng):
            run("sp", eng)


class Ctx:
    def __init__(self, nc, st):
        self.nc = nc
        self.st = st
        self.S = Sched(nc)

    def sb(self, name, shape, dt=F32):
        return self.st.enter_context(self.nc.sbuf_tensor(name, shape, dt))

    def ps(self, name, shape, dt=F32):
        return self.st.enter_context(self.nc.psum_tensor(name, shape, dt))

    def din(self, name, shape, dt=F32):
        return self.nc.dram_tensor(name, list(shape), dt, kind="ExternalInput").ap()

    def dout(self, name, shape, dt=F32):
        return self.nc.dram_tensor(name, list(shape), dt, kind="ExternalOutput").ap()

    def dint(self, name, shape, dt=F32):
        return self.nc.dram_tensor(name, list(shape), dt, kind="Internal").ap()

    def dma(self, out, in_, r=(), w=(), q="sp"):
        self.S.add(q, lambda e: e.dma_start(out=out, in_=in_), r, w, dma=True)

    def mm(self, out, lhsT, rhs, start, stop, r, w):
        self.S.add("pe", lambda e: e.matmul(out, lhsT, rhs, start=start, stop=stop), r, w)

    def tr(self, out, in_, ident, r, w):
        self.S.add("pe", lambda e: e.transpose(out, in_, ident), r, w)

    def act(self, out, in_, func, r, w, bias=None, scale=None, accum=None):
        kw = {}
        if bias is not None:
            kw["bias"] = bias
        if scale is not None:
            kw["scale"] = scale
        if accum is not None:
            kw["accum_out"] = accum
        self.S.add("act", lambda e: e.activation(out=out, in_=in_, func=func, **kw), r, w)

    def ts(self, eng, out, in0, s1, s2, op0, op1, r, w):
        if op1 is None:
            self.S.add(eng, lambda e: e.tensor_scalar(out=out, in0=in0, scalar1=s1, scalar2=None, op0=op0), r, w)
        else:
            self.S.add(eng, lambda e: e.tensor_scalar(out=out, in0=in0, scalar1=s1, scalar2=s2, op0=op0, op1=op1), r, w)

    def stt(self, eng, out, in0, scalar, in1, op0, op1, r, w):
        self.S.add(eng, lambda e: e.scalar_tensor_tensor(out=out, in0=in0, scalar=scalar, in1=in1, op0=op0, op1=op1), r, w)

    def tt(self, eng, out, in0, in1, op, r, w):
        self.S.add(eng, lambda e: e.tensor_tensor(out=out, in0=in0, in1=in1, op=op), r, w)

    def cp(self, eng, out, in_, r, w):
        if eng == "act":
            self.S.add(eng, lambda e: e.copy(out=out, in_=in_), r, w)
        else:
            self.S.add(eng, lambda e: e.tensor_copy(out=out, in_=in_), r, w)

    def memset(self, eng, ap, val, w):
        self.S.add(eng, lambda e: e.memset(ap, val), (), w)

    def recip(self, out, in_, r, w):
        self.S.add("dve", lambda e: e.reciprocal(out=out, in_=in_), r, w)


class Pool:
    def __init__(self, cx, name, n, shape, dt, psum=False):
        self.t = [(cx.ps if psum else cx.sb)("%s%d" % (name, i), shape, dt) for i in range(n)]
        self.k = [(name, i) for i in range(n)]
        self.i = 0
        if psum:
            cx.S.psum_keys.update(self.k)

    def get(self):
        j = self.i % len(self.t)
        self.i += 1
        return self.t[j], self.k[j]


def build_A():
    nc = bass.Bass("TRN2", target_bir_lowering=False)
    with ExitStack() as st:
        cx = Ctx(nc, st)
        S = cx.S
        x_d = cx.din("x", [SEQ, D])
        c_d = cx.din("c", [128, 8])
        pos_d = cx.din("pos", [1, SEQ], I32)
        ng_d = cx.din("norm_g", [128, 8])
        wada_d = cx.din("w_ada", [D, 2048])
        bada_d = cx.din("b_ada", [128, 16])
        win_d = cx.din("w_in", [D, NCOL])
        qng_d = cx.din("qng", [128, 2])
        wuq_d = cx.din("w_uq", [256, 384])
        kvng_d = cx.din("kvng", [128, 1])
        wukv_d = cx.din("w_ukv", [128, 256])
        gv_d = cx.din("gvec", [96, 8])
        rb_d = cx.din("rb", [32, 2])
        ident_d = cx.din("ident", [128, 128], BF16)
        tri_d = cx.din("tri", [128, 128], BF16)
        blk_d = cx.din("blkoh", [32, SEQ], BF16)
        oh_d = cx.din("bucket_oh", [32, WZ])
        msk_d = cx.din("masks", [3, 1024])
        mix_d = cx.dout("mixT", [256, SEQ], BF16)
        ef_d = cx.dint("ef_scr", [2, WZ], BF16)
        z_d = cx.dint("z_scr", [2, 128, WZ], BF16)

        KT = [cx.sb("KT%d" % h, [96, SEQ], BF16) for h in range(4)]
        Vall = cx.sb("Vall", [128, 4, SEQ // 128, 65], BF16)
        Rt = [cx.sb("Rt%d" % h, [128, LR], BF16) for h in range(2)]
        Wsb = cx.sb("Wsb", [128, 8, NCOL], BF16)
        Wkr = cx.sb("Wkr", [128, 8, 96], BF16)
        WkrS = cx.sb("WkrS", [128, 8, 96], BF16)
        Wuq = cx.sb("Wuq", [128, 2, 384], BF16)
        Wkn = cx.sb("Wkn", [128, 2, 96], BF16)
        Wv = cx.sb("Wv", [128, 128], BF16)
        ident = cx.sb("ident_sb", [128, 128], BF16)
        tri = cx.sb("tri_sb", [128, 128], BF16)
        ones_b = cx.sb("ones_b", [128, 128], BF16)
        ones_f = cx.sb("ones_f", [128, 64], F32)
        masks = cx.sb("masks_sb", [128, 3, 64], F32)
        gv = cx.sb("gv", [96, 8], F32)
        gd = cx.sb("gd", [96, 8], F32)
        b31 = cx.sb("b31", [128, 2], F32)
        kmean = [cx.sb("kmean%d" % h, [64, 32], F32) for h in range(2)]
        csb = cx.sb("c_sb", [128, 8], F32)
        silc = cx.sb("silc", [128, 8], F32)
        ngs = cx.sb("ng_sb", [128, 8], F32)
        bada = cx.sb("bada_sb", [128, 16], F32)
        acc = cx.sb("acc", [128, 16], F32)
        avec = cx.sb("avec", [128, 8], F32)
        qng = cx.sb("qng_sb", [128, 2], F32)
        kvng = cx.sb("kvng_sb", [128, 1], F32)
        rbs = cx.sb("rb_sb", [32, 2], F32)
        efs = cx.sb("ef_sb", [2, 512], BF16)
        xs = Pool(cx, "xs", 2, [128, NCOL], F32)
        junk = cx.sb("junk", [128, 1024], BF16)
        xn = [cx.sb("xn%d" % s, [128, 1024], BF16) for s in range(4)]
        hT = cx.sb("hT", [128, 8, T], BF16)
        cqn = cx.sb("cqn", [128, 2, T], BF16)
        ckvn = cx.sb("ckvn", [128, T], BF16)
        Ctab = cx.sb("Ctab", [96, T], F32)
        Stab = cx.sb("Stab", [96, T], F32)
        sg = [cx.sb("sg%d" % h, [64, T], BF16) for h in range(4)]
        QT = [cx.sb("QT%d" % h, [96, T], BF16) for h in range(4)]
        posi = cx.sb("posi", [96, T], I32)
        Mpad = [cx.sb("Mpad%d" % s, [128, 96], BF16) for s in range(4)]
        wk = Pool(cx, "wk", 8, [128, T], F32)
        wb = Pool(cx, "wb", 3, [128, T], BF16)
        PT = Pool(cx, "PT", 4, [128, T], BF16)
        mixo = Pool(cx, "mixo", 2, [64, T], BF16)
        sm = Pool(cx, "sm", 8, [128, 64], F32)
        PG = Pool(cx, "PG", 4, [128, T], F32, psum=True)
        PS_ = Pool(cx, "PS", 2, [128, T], F32, psum=True)
        PO = Pool(cx, "PO", 2, [128, T], F32, psum=True)

        cx.dma(ident[:], ident_d, w=["ident"])
        cx.dma(tri[:], tri_d, w=["tri"])
        cx.memset("pool", ones_b[:], 1.0, ["ones_b"])
        cx.memset("pool", ones_f[:], 1.0, ["ones_f"])
        cx.dma(gv[:], gv_d, w=["gv"])
        cx.dma(b31[:], bass.AP(rb_d.tensor, 31 * 2, [[0, 128], [1, 2]]), w=["b31"])
        cx.dma(rbs[:], rb_d, w=["rbs"])
        sc_mla = 96.0 ** -0.5
        sc_mob = 64.0 ** -0.5
        cx.ts("dve", gd[:, 0:1], gv[:, 0:1], sc_mla, None, ALU.mult, None, ["gv"], [("gd", 0)])
        cx.stt("dve", gd[:, 1:2], gv[:, 1:2], sc_mla, gv[:, 4:5], ALU.mult, ALU.mult, ["gv"], [("gd", 1)])
        cx.cp("dve", gd[:, 2:3], gv[:, 2:3], ["gv"], [("gd", 2)])
        cx.tt("dve", gd[:, 3:4], gv[:, 3:4], gv[:, 4:5], ALU.mult, ["gv"], [("gd", 3)])
        cx.ts("dve", gd[0:64, 4:5], gv[0:64, 6:7], sc_mob, None, ALU.mult, None, ["gv"], [("gd", 4)])
        cx.cp("dve", gd[0:64, 5:6], gv[0:64, 7:8], ["gv"], [("gd", 5)])
        for h in range(2):
            cx.memset("pool", kmean[h][:], 0.0, [("kmean", h)])
        cx.memset("pool", Vall[:], 1.0, [("V", i) for i in range(NT)])
        cx.memset("pool", Wkr[:], 0.0, ["Wkr"])
        cx.memset("pool", WkrS[:], 0.0, ["WkrS"])
        cx.memset("pool", Wkn[:], 0.0, ["Wkn"])
        for s in range(4):
            cx.memset("pool", Mpad[s][:], 0.0, [("Mpad", s)])
        for h in range(2):
            cx.dma(KT[2 + h][64:96, :], blk_d, w=[("KTaug", h)])

        cx.dma(csb[:], c_d, w=["csb"])
        cx.dma(ngs[:], ng_d, w=["ngs"])
        cx.dma(bada[:], bada_d, w=["bada"])
        cx.act(silc[:], csb[:], AF.Silu, ["csb"], ["silc"])
        cx.memset("dve", acc[:], 0.0, ["acc"])
        for k in range(8):
            stg = []
            for half in range(2):
                tl, ky = xs.get()
                cx.dma(tl[:, 0:1024], wada_d[k * 128:(k + 1) * 128, half * 1024:(half + 1) * 1024], w=[ky])
                stg.append((tl, ky))
            pg, pk = PG.get()
            for j in range(16):
                tl, ky = stg[j // 8]
                cx.mm(pg[:, j:j + 1], tl[:, (j % 8) * 128:(j % 8 + 1) * 128], silc[:, k:k + 1], True, True,
                      [ky, "silc"], [pk])
            cx.tt("dve", acc[:], acc[:], pg[:, 0:16], ALU.add, ["acc", pk], ["acc"])
        cx.tt("dve", acc[:], acc[:], bada[:], ALU.add, ["acc", "bada"], ["acc"])
        cx.stt("dve", avec[:], acc[:, 8:16], 1.0, ngs[:], ALU.add, ALU.mult, ["acc", "ngs"], ["avec"])
        SHIFT = acc

        for k in range(8):
            tl, ky = xs.get()
            cx.dma(tl[:, :], win_d[k * 128:(k + 1) * 128, :], w=[ky])
            cx.cp(("dve", "pool")[k % 2], Wsb[:, k, :], tl[:, :], [ky], [("Wsb", k)])
            cx.cp("act", Wkr[:, k, 64:96], tl[:, 384:416], [ky, "Wkr"], [("Wkr", k)])
            cx.cp("act", WkrS[:, k, 64:96], tl[:, 416:448], [ky, "WkrS"], [("WkrS", k)])
        WSB = [("Wsb", k) for k in range(8)]
        WKR = [("Wkr", k) for k in range(8)]
        WKRS = [("WkrS", k) for k in range(8)]
        cx.dma(qng[:], qng_d, w=["qng"])
        cx.dma(kvng[:], kvng_d, w=["kvng"])
        for k in range(2):
            tl, ky = xs.get()
            cx.dma(tl[:, 0:384], wuq_d[k * 128:(k + 1) * 128, :], w=[ky])
            cx.ts("dve", Wuq[:, k, :], tl[:, 0:384], qng[:, k:k + 1], None, ALU.mult, None, [ky, "qng"], [("Wuq", k)])
        tl, ky = xs.get()
        cx.dma(tl[:, 0:256], wukv_d, w=[ky])
        for h in range(2):
            cx.ts("dve", Wkn[:, h, 0:64], tl[:, h * 64:(h + 1) * 64], kvng[:, 0:1], None, ALU.mult, None,
                  [ky, "kvng", "Wkn"], [("Wkn", h)])
        cx.ts("dve", Wv[:], tl[:, 128:256], kvng[:, 0:1], None, ALU.mult, None, [ky, "kvng"], ["Wv"])

        nchunk = WZ // 512
        for cidx in range(nchunk):
            tl, ky = xs.get()
            cx.dma(tl[0:32, 0:512], oh_d[:, cidx * 512:(cidx + 1) * 512], w=[ky])
            pg, pk = PG.get()
            cx.mm(pg[0:2, :], rbs[0:32, 0:2], tl[0:32, 0:512], True, True, [ky, "rbs"], [pk])
            cx.act(efs[0:2, :], pg[0:2, :], AF.Exp, [pk], ["efs"])
            if cidx == 0:
                cx.memset("dve", efs[0:2, 0:127], 0.0, ["efs"])
            cx.dma(ef_d[:, cidx * 512:(cidx + 1) * 512], efs[0:2, :], r=["efs"], w=["ef_d"])
        for h in range(2):
            cx.dma(z_d[h], bass.AP(ef_d.tensor, h * WZ, [[0, 128], [1, WZ]]), r=["ef_d"], w=[("z_d", h)])
            cx.dma(Rt[h][:], bass.AP(z_d.tensor, h * 128 * WZ + 127, [[WZ - 1, 128], [1, LR]]),
                   r=[("z_d", h)], w=[("Rt", h)])

        def rsqrt_from(psum_ap, n, inv_n, rk):
            rt, rtk = wk.get()
            cx.act(rt[0:n, :], psum_ap, AF.Sqrt, rk, [rtk], bias=epsc[0:n, 0:1], scale=inv_n)
            rs, rsk = wk.get()
            cx.recip(rs[0:n, :], rt[0:n, :], [rtk], [rsk])
            return rs, rsk

        epsc = cx.sb("epsc", [128, 1], F32)
        cx.memset("pool", epsc[:], EPS, ["epsc"])
        negpi = cx.sb("negpi", [128, 1], F32)
        cx.memset("pool", negpi[:], 0.0, ["negpi"])

        def proj(lo, n, extra_first=None):
            pg, pk = PG.get()
            first = True
            if extra_first is not None:
                lhsT, rhs, rk = extra_first
                cx.mm(pg[0:n, :], lhsT, rhs, True, False, rk, [pk])
                first = False
            for k in range(8):
                cx.mm(pg[0:n, :], Wsb[:, k, lo:lo + n], hT[:, k, :], first and k == 0, k == 7,
                      [("Wsb", k), ("hT", k)], [pk])
            return pg, pk

        def norm_stats(pa, pak, n, inv_n):
            sq, sqk = wb.get()
            cx.act(sq[0:n, :], pa[0:n, :], AF.Square, [pak], [sqk])
            pc, pck = PG.get()
            cx.mm(pc[0:n, :], ones_b[0:n, 0:n], sq[0:n, :], True, True, ["ones_b", sqk], [pck])
            return rsqrt_from(pc[0:n, :], n, inv_n, [pck, "epsc"])

        def rope_side(pa, pak, pb, pbk, gc, gs, out_ap, outk):
            rs, rsk = norm_stats(pa, pak, 96, 1.0 / 96.0)
            t1, t1k = wk.get()
            cx.stt("dve", t1[0:96, :], pa[0:96, :], gd[:, gc:gc + 1], Ctab[:], ALU.mult, ALU.mult,
                   [pak, ("gd", gc), "Ctab"], [t1k])
            t2, t2k = wk.get()
            cx.stt("dve", t2[0:96, :], pb[0:96, :], gd[:, gs:gs + 1], Stab[:], ALU.mult, ALU.mult,
                   [pbk, ("gd", gs), "Stab"], [t2k])
            cx.tt("pool", t1[0:96, :], t1[0:96, :], t2[0:96, :], ALU.add, [t1k, t2k], [t1k])
            cx.tt("dve", out_ap, t1[0:96, :], rs[0:96, :], ALU.mult, [t1k, rsk], outk)

        for i in range(NT):
            t0 = i * T
            cx.dma(posi[:], bass.AP(pos_d.tensor, t0, [[0, 96], [1, T]]), w=["posi"])
            posf, posfk = wk.get()
            cx.cp("pool", posf[0:96, :], posi[:], ["posi"], [posfk])
            for (tab, tabk, phase) in ((Ctab, "Ctab", math.pi / 2), (Stab, "Stab", 0.0)):
                y, yk = wk.get()
                cx.ts("pool", y[0:96, :], posf[0:96, :], gv[:, 5:6], phase, ALU.mult, ALU.add, [posfk, "gv"], [yk])
                ki, kik = wk.get()
                kiv = ki[0:96, :].bitcast(I32)
                cx.ts("pool", kiv, y[0:96, :], 1.0 / TWO_PI, None, ALU.mult, None, [yk], [kik])
                kf, kfk = wk.get()
                cx.cp("pool", kf[0:96, :], kiv, [kik], [kfk])
                cx.stt("dve", y[0:96, :], kf[0:96, :], -CW1, y[0:96, :], ALU.mult, ALU.add, [kfk, yk], [yk])
                cx.stt("dve", y[0:96, :], kf[0:96, :], -CW2, y[0:96, :], ALU.mult, ALU.add, [kfk, yk], [yk])
                cx.ts("pool", y[0:96, :], y[0:96, :], -math.pi, math.pi, ALU.max, ALU.min, [yk], [yk])
                cx.act(tab[:], y[0:96, :], AF.Sin, [yk], [tabk])

            for s in range(4):
                tl, ky = xs.get()
                cx.dma(tl[:, 0:1024], x_d[t0 + s * 128: t0 + (s + 1) * 128, :], w=[ky])
                ss, ssk = sm.get()
                cx.act(junk[:], tl[:, 0:1024], AF.Square, [ky], ["junk", ssk], accum=ss[:, 0:1])
                cx.act(ss[:, 1:2], ss[:, 0:1], AF.Sqrt, [ssk, "epsc"], [ssk], bias=epsc[:, 0:1], scale=1.0 / D)
                cx.recip(ss[:, 2:3], ss[:, 1:2], [ssk], [ssk])
                cx.ts(("dve", "pool")[s % 2], xn[s][:], tl[:, 0:1024], ss[:, 2:3], None, ALU.mult, None,
                      [ky, ssk], [("xn", s)])
            for k in range(8):
                trp, trk = PG.get()
                trv = trp[:].bitcast(BF16)
                for s in range(4):
                    cx.tr(trv[:, s * 128:(s + 1) * 128], xn[s][:, k * 128:(k + 1) * 128],
                          ident[:], [("xn", s), "ident"], [trk])
                if k % 2 == 0:
                    cx.act(hT[:, k, :], trv[:, 0:512], AF.Identity, [trk, "avec", "acc"],
                           [("hT", k)], bias=SHIFT[:, k:k + 1], scale=avec[:, k:k + 1])
                else:
                    cx.ts("dve", hT[:, k, :], trv[:, 0:512], avec[:, k:k + 1], SHIFT[:, k:k + 1],
                          ALU.mult, ALU.add, [trk, "avec", "acc"], [("hT", k)])

            pa, pak = proj(0, 128)
            pb, pbk = proj(128, 128)
            sq0, sq0k = wb.get()
            cx.act(sq0[:], pa[:], AF.Square, [pak], [sq0k])
            sq1, sq1k = wb.get()
            cx.act(sq1[:], pb[:], AF.Square, [pbk], [sq1k])
            pc, pck = PG.get()
            cx.mm(pc[:], ones_b[:], sq0[:], True, False, ["ones_b", sq0k], [pck])
            cx.mm(pc[:], ones_b[:], sq1[:], False, True, ["ones_b", sq1k], [pck])
            rs, rsk = rsqrt_from(pc[:], 128, 1.0 / 256.0, [pck, "epsc"])
            cx.tt("dve", cqn[:, 0, :], pa[:], rs[:], ALU.mult, [pak, rsk], [("cqn", 0)])
            cx.tt("dve", cqn[:, 1, :], pb[:], rs[:], ALU.mult, [pbk, rsk], [("cqn", 1)])
            pa, pak = proj(256, 128)
            rs, rsk = norm_stats(pa, pak, 128, 1.0 / 128.0)
            cx.tt("dve", ckvn[:], pa[:], rs[:], ALU.mult, [pak, rsk], ["ckvn"])
            for hh, lo in enumerate((448, 512, 960, 1024)):
                pa, pak = proj(lo, 64)
                cx.act(sg[hh][:], pa[0:64, :], AF.Silu, [pak], [("sg", hh)])
            for h in range(2):
                pa, pak = PG.get()
                for k in range(2):
                    cx.mm(pa[0:96, :], Wuq[:, k, h * 96:(h + 1) * 96], cqn[:, k, :], k == 0, k == 1,
                          [("Wuq", k), ("cqn", k)], [pak])
                pb, pbk = PG.get()
                for k in range(2):
                    cx.mm(pb[0:96, :], Wuq[:, k, 192 + h * 96:192 + (h + 1) * 96], cqn[:, k, :], k == 0, k == 1,
                          [("Wuq", k), ("cqn", k)], [pbk])
                rope_side(pa, pak, pb, pbk, 0, 1, QT[h][:], [("QT", h)])
            for h in range(2):
                pa, pak = PG.get()
                cx.mm(pa[0:96, :], Wkn[:, h, :], ckvn[:], True, False, [("Wkn", h), "ckvn"], [pak])
                for k in range(8):
                    cx.mm(pa[0:96, :], Wkr[:, k, :], hT[:, k, :], False, k == 7, [("Wkr", k), ("hT", k)], [pak])
                pb, pbk = PG.get()
                for k in range(8):
                    cx.mm(pb[0:96, :], WkrS[:, k, :], hT[:, k, :], k == 0, k == 7, [("WkrS", k), ("hT", k)], [pbk])
                rope_side(pa, pak, pb, pbk, 2, 3, KT[h][:, t0:t0 + T], [("KT", h, i)])
            pa, pak = PG.get()
            for s in range(4):
                cx.mm(pa[:, s * 128:(s + 1) * 128], ckvn[:, s * 128:(s + 1) * 128], Wv[:], True, True,
                      ["ckvn", "Wv"], [pak])
            pav = pa[:].rearrange("p (s h d) -> p s h d", s=4, h=2)
            for h in range(2):
                cx.cp(("act", "dve")[h], Vall[:, h, 4 * i:4 * i + 4, 0:64], pav[:, :, h, :], [pak], [("V", i)])
            for m in range(3):
                cx.dma(masks[:, m, :], bass.AP(msk_d.tensor, m * 1024 + 2 * i * 32, [[0, 128], [1, 64]]),
                       w=[("masks", m)])
            for h in range(2):
                pa, pak = proj(704 + 64 * h, 64)
                rs, rsk = norm_stats(pa, pak, 64, 1.0 / 64.0)
                kn, knk = wk.get()
                cx.stt("dve", kn[0:64, :], pa[0:64, :], gd[0:64, 5:6], rs[0:64, :], ALU.mult, ALU.mult,
                       [pak, ("gd", 5), rsk], [knk])
                cx.cp("pool", KT[2 + h][0:64, t0:t0 + T], kn[0:64, :], [knk], [("KT", 2 + h, i)])
                km, kmk = sm.get()
                cx.S.add("dve", lambda e, km=km, kn=kn: e.tensor_reduce(
                    out=km[0:64, 0:2], in_=kn[0:64, :].rearrange("p (a b) -> p a b", a=2), axis=AX.X, op=ALU.add),
                    [knk], [kmk])
                cx.ts("dve", kmean[h][:, 2 * i:2 * i + 2], km[0:64, 0:2], 1.0 / 256.0, None, ALU.mult, None,
                      [kmk, ("kmean", h)], [("kmean", h)])
                pa, pak = proj(576 + 64 * h, 64)
                rs, rsk = norm_stats(pa, pak, 64, 1.0 / 64.0)
                qn, qnk = wk.get()
                cx.stt("dve", qn[0:64, :], pa[0:64, :], gd[0:64, 4:5], rs[0:64, :], ALU.mult, ALU.mult,
                       [pak, ("gd", 4), rsk], [qnk])
                cx.cp("pool", QT[2 + h][0:64, :], qn[0:64, :], [qnk], [("QT", 2 + h)])
                pg, pgk = PG.get()
                for s in range(4):
                    cx.mm(pg[:, s * 32:(s + 1) * 32], qn[0:64, s * 128:(s + 1) * 128], kmean[h][:, :], True, True,
                          [qnk, ("kmean", h)], [pgk])
                pm, pmk = PG.get()
                for s in range(4):
                    own = 2 * i + s // 2
                    osl = slice((s // 2) * 32, (s // 2 + 1) * 32)
                    gm, gmk = sm.get()
                    cx.tt("dve", gm[:, 0:32], pg[:, s * 32:(s + 1) * 32], masks[:, 0, osl], ALU.add,
                          [pgk, ("masks", 0)], [gmk])
                    cx.S.add("dve", lambda e, gm=gm: e.max(out=gm[:, 32:40], in_=gm[:, 0:32]), [gmk], [gmk])
                    cx.ts("dve", gm[:, 0:32], gm[:, 0:32], gm[:, 34:35], None, ALU.is_ge, None, [gmk], [gmk])
                    cx.ts("dve", gm[:, 0:32], gm[:, 0:32], BIG, -BIG, ALU.mult, ALU.add, [gmk], [gmk])
                    cx.tt("dve", gm[:, 0:32], gm[:, 0:32], masks[:, 1, osl], ALU.max, [gmk, ("masks", 1)], [gmk])
                    cx.tt("dve", Mpad[s][:, 64:96], gm[:, 0:32], masks[:, 2, osl], ALU.add,
                          [gmk, ("masks", 2), ("Mpad", s)], [("Mpad", s)])
                    cx.mm(pm[0:96, s * 128:(s + 1) * 128], Mpad[s][:, 0:96], ident[:], True, True,
                          [("Mpad", s), "ident"], [pmk])
                cx.cp("act", QT[2 + h][64:96, :], pm[64:96, :], [pmk], [("QTm", h)])
            pa, pak = PG.get()
            for s in range(4):
                for k in range(8):
                    cx.mm(pa[:, s * 128:(s + 1) * 128], hT[:, k, s * 128:(s + 1) * 128], Wsb[:, k, 832:960],
                          k == 0, k == 7, [("hT", k), ("Wsb", k)], [pak])
            pav = pa[:].rearrange("p (s h d) -> p s h d", s=4, h=2)
            for h in range(2):
                cx.cp(("act", "dve")[h], Vall[:, 2 + h, 4 * i:4 * i + 4, 0:64], pav[:, :, h, :], [pak], [("V", i)])

            LAG = 2
            steps = [(hh, kt) for hh in range(4) for kt in range(4 * i + 4)]
            pend = []
            obank = {}

            def do_pv(item):
                hh, kt, c0, pt, ptk = item
                po, pok = obank[hh]
                nk = 4 * i + 4
                cx.mm(po[0:65, c0:T], Vall[:, hh, kt, 0:65], pt[:, c0:T], kt == 0, kt == nk - 1,
                      [("V", kt // 4), ptk], [pok])
                if kt == nk - 1:
                    lr, lrk = wk.get()
                    cx.cp("act", lr[64:65, :], po[64:65, :], [pok], [lrk])
                    cx.recip(lr[64:65, :], lr[64:65, :], [lrk], [lrk])
                    pbc, pbck = PG.get()
                    cx.mm(pbc[0:64, :], ones_f[64:65, 0:64], lr[64:65, :], True, True, ["ones_f", lrk], [pbck])
                    o1, o1k = wk.get()
                    cx.tt("dve", o1[0:64, :], po[0:64, :], sg[hh][:], ALU.mult, [pok, ("sg", hh)], [o1k])
                    mo, mok = mixo.get()
                    cx.tt("dve", mo[:], o1[0:64, :], pbc[0:64, :], ALU.mult, [o1k, pbck], [mok])
                    cx.dma(mix_d[hh * 64:(hh + 1) * 64, t0:t0 + T], mo[:], r=[mok], w=[("mix", hh, i)])

            for (hh, kt) in steps:
                if kt == 0:
                    obank[hh] = PO.get()
                r = kt - 4 * i
                c0 = 128 * r if r > 0 else 0
                delta = t0 - kt * 128
                moba = hh >= 2
                ps_, psk = PS_.get()
                kr = [("KT", hh, kt // 4), ("QT", hh)]
                if moba:
                    kr += [("KTaug", hh - 2), ("QTm", hh - 2)]
                cx.mm(ps_[:, c0:T], KT[hh][:, kt * 128:(kt + 1) * 128], QT[hh][:, c0:T], True, True, kr, [psk])
                pt, ptk = PT.get()
                if moba and delta >= FAR:
                    cx.act(pt[:, c0:T], ps_[:, c0:T], AF.Exp, [psk, "b31"], [ptk], bias=b31[:, hh - 2:hh - 1])
                else:
                    cx.act(pt[:, c0:T], ps_[:, c0:T], AF.Exp, [psk], [ptk])
                    if moba:
                        f0 = delta + c0
                        cx.tt("dve", pt[:, c0:T], pt[:, c0:T], Rt[hh - 2][:, f0:f0 + (T - c0)], ALU.mult,
                              [ptk, ("Rt", hh - 2)], [ptk])
                    elif r >= 0:
                        cx.tt("dve", pt[:, c0:c0 + 128], pt[:, c0:c0 + 128], tri[:], ALU.mult, [ptk, "tri"], [ptk])
                pend.append((hh, kt, c0, pt, ptk))
                if len(pend) > LAG:
                    do_pv(pend.pop(0))
            while pend:
                do_pv(pend.pop(0))

        cx.S.add("sp", lambda e: None, [("mix", hh, i) for hh in range(4) for i in range(NT)], ())
        S.emit(st)
    return nc


TB = 2048


def build_B():
    nc = bass.Bass("TRN2", target_bir_lowering=False)
    with ExitStack() as st:
        cx = Ctx(nc, st)
        S = cx.S
        x_d = cx.din("x", [TB, D])
        mix_d = cx.din("mixT", [D, TB], BF16)
        wo_d = cx.din("w_out", [D, D])
        c_d = cx.din("c", [128, 8])
        wg_d = cx.din("w_ada_g", [D, D])
        bg_d = cx.din("b_ada_g", [1, D])
        out_d = cx.dout("xo", [TB, D])

        mixs = cx.sb("mixs", [128, 8, TB], BF16)
        Wo = cx.sb("Wo", [128, 8, D], BF16)
        gate = cx.sb("gate", [128, D], F32)
        bgb = cx.sb("bgb", [128, D], F32)
        csb = cx.sb("csb", [128, 8], F32)
        silc = cx.sb("silc", [128, 8], F32)
        scb = cx.sb("scb", [128, 8, 128], F32)
        stg = Pool(cx, "stg", 3, [128, D], F32)
        xo = Pool(cx, "xo", 2, [128, D], F32)
        PG = Pool(cx, "PG", 4, [128, 512], F32, psum=True)

        cx.dma(csb[:], c_d, w=["csb"])
        cx.dma(bgb[:], bass.AP(bg_d.tensor, 0, [[0, 128], [1, D]]), w=["bgb"])
        for f in range(8):
            cx.dma(mixs[:, f, :], mix_d[f * 128:(f + 1) * 128, :], w=[("mixs", f)])
        cx.act(silc[:], csb[:], AF.Silu, ["csb"], ["silc"])
        for k in range(8):
            cx.cp("dve", scb[:, k, :], silc[:, k:k + 1].to_broadcast([128, 128]), ["silc"], [("scb", k)])
        pgs = [PG.get(), PG.get()]
        for k in range(8):
            tl, ky = stg.get()
            cx.dma(tl[:], wg_d[k * 128:(k + 1) * 128, :], w=[ky])
            for half in range(2):
                cx.mm(pgs[half][0][:], scb[:, k, :], tl[:, half * 512:(half + 1) * 512], k == 0, k == 7,
                      [("scb", k), ky], [pgs[half][1]])
        for half in range(2):
            cx.tt("dve", gate[:, half * 512:(half + 1) * 512], pgs[half][0][:], bgb[:, half * 512:(half + 1) * 512],
                  ALU.add, [pgs[half][1], "bgb"], ["gate"])
        for f in range(8):
            tl, ky = stg.get()
            cx.dma(tl[:], wo_d[f * 128:(f + 1) * 128, :], w=[ky])
            cx.tt(("dve", "pool")[f % 2], Wo[:, f, :], tl[:], gate[:], ALU.mult, [ky, "gate"], [("Wo", f)])
        for s in range(TB // 128):
            tl, ky = stg.get()
            cx.dma(tl[:], x_d[s * 128:(s + 1) * 128, :], w=[ky])
            ot, otk = xo.get()
            for half in range(2):
                pg, pk = PG.get()
                for f in range(8):
                    cx.mm(pg[:], mixs[:, f, s * 128:(s + 1) * 128], Wo[:, f, half * 512:(half + 1) * 512],
                          f == 0, f == 7, [("mixs", f), ("Wo", f)], [pk])
                cx.tt("dve", ot[:, half * 512:(half + 1) * 512], pg[:], tl[:, half * 512:(half + 1) * 512], ALU.add,
                      [pk, ky], [otk])
            cx.dma(out_d[s * 128:(s + 1) * 128, :], ot[:], r=[otk], w=[("out", s)])
        cx.S.add("sp", lambda e: None, [("out", s) for s in range(TB // 128)], ())
        S.emit(st)
    return nc


def _rel_bucket_np(dist):
    n = np.maximum(dist, 0)
    nf = np.maximum(n, 1).astype(np.float32)
    large = 16 + (np.log(nf / np.float32(16)) / np.float32(math.log(4096 / 16)) * np.float32(16)).astype(np.int32)
    large = np.minimum(large, 31)
    return np.where(n < 16, n, large)


def _constants():
    bf = ml_dtypes.bfloat16
    ident = np.eye(128, dtype=np.float32).astype(bf)
    tri = (np.arange(128)[None, :] >= np.arange(128)[:, None]).astype(np.float32).astype(bf)
    blk = np.zeros((32, SEQ), np.float32)
    for j in range(32):
        blk[j, j * 256:(j + 1) * 256] = 1.0
    blk = blk.astype(bf)
    dist = np.arange(WZ) - 127
    bk = _rel_bucket_np(dist)
    oh = np.zeros((32, WZ), np.float32)
    valid = dist >= 0
    oh[bk[valid], np.nonzero(valid)[0]] = 1.0
    own = np.arange(32)[:, None]
    j = np.arange(32)[None, :]
    neg = np.where(j >= own, -1e30, 0.0).astype(np.float32).reshape(1, 1024)
    ownfix = np.where(j == own, 0.0, -BIG).astype(np.float32).reshape(1, 1024)
    neg2 = np.where(j > own, -BIG, 0.0).astype(np.float32).reshape(1, 1024)
    masks = np.concatenate([neg, ownfix, neg2], axis=0)
    sgn = np.zeros(96, np.float32)
    sgn[64:80] = -1.0
    sgn[80:96] = 1.0
    half = 16
    inv_freq = (np.float32(10000.0) ** (-np.arange(0, half, dtype=np.float32) / np.float32(half))).astype(np.float32)
    invf = np.zeros(96, np.float32)
    invf[64:80] = inv_freq
    invf[80:96] = inv_freq
    return dict(ident=ident, tri=tri, blkoh=blk, bucket_oh=oh, masks=masks), sgn, invf


def _col(v, nk):
    return np.ascontiguousarray(np.asarray(v, np.float32).reshape(nk, 128).T)


def _perm_rope(v):
    p = np.array(v, np.float32).copy()
    p[64:80] = v[80:96]
    p[80:96] = v[64:80]
    return p


_PROG = {}


def _prog(name):
    if name not in _PROG:
        _PROG[name] = build_A() if name == "A" else build_B()
    return _PROG[name]


def _layer_inputs_A(l, b, hg, xb, inp, consts, sgn, invf):
    f = lambda a: np.asarray(a, np.float32)
    w_in = f(inp["w_in"][l])
    mh = [2 * hg, 2 * hg + 1]
    cols = [w_in[:, 0:256], w_in[:, 256:384], w_in[:, 384:416]]
    kr = w_in[:, 384:416]
    cols.append(np.concatenate([kr[:, 16:32], kr[:, 0:16]], axis=1))
    o_gm, o_qb, o_kb, o_vb, o_gb = 416, 928, 1440, 1952, 2464
    for h in mh:
        cols.append(w_in[:, o_gm + 64 * h:o_gm + 64 * (h + 1)])
    for h in mh:
        cols.append(w_in[:, o_qb + 64 * h:o_qb + 64 * (h + 1)])
    for h in mh:
        cols.append(w_in[:, o_kb + 64 * h:o_kb + 64 * (h + 1)])
    for h in mh:
        cols.append(w_in[:, o_vb + 64 * h:o_vb + 64 * (h + 1)])
    for h in mh:
        cols.append(w_in[:, o_gb + 64 * h:o_gb + 64 * (h + 1)])
    w_in_c = np.ascontiguousarray(np.concatenate(cols, axis=1))
    assert w_in_c.shape[1] == NCOL
    wuq = f(inp["mla_w_uq"][l])
    uq = []
    for h in mh:
        uq.append(wuq[:, h * 96:(h + 1) * 96])
    for h in mh:
        blk = wuq[:, h * 96:(h + 1) * 96]
        uq.append(np.concatenate([blk[:, 0:64], blk[:, 80:96], blk[:, 64:80]], axis=1))
    wuq_c = np.ascontiguousarray(np.concatenate(uq, axis=1))
    wukv = f(inp["mla_w_ukv"][l])
    ukv = [wukv[:, h * 128:h * 128 + 64] for h in mh] + [wukv[:, h * 128 + 64:(h + 1) * 128] for h in mh]
    wukv_c = np.ascontiguousarray(np.concatenate(ukv, axis=1))
    gq = f(inp["mla_q_g"][l])
    gk = f(inp["mla_k_g"][l])
    gvec = np.zeros((96, 8), np.float32)
    gvec[:, 0] = gq
    gvec[:, 1] = _perm_rope(gq)
    gvec[:, 2] = gk
    gvec[:, 3] = _perm_rope(gk)
    gvec[:, 4] = sgn
    gvec[:, 5] = invf
    gvec[0:64, 6] = f(inp["moba_q_g"][l])
    gvec[0:64, 7] = f(inp["moba_k_g"][l])
    d = dict(
        x=np.ascontiguousarray(xb),
        c=_col(inp["c"][b], 8),
        pos=np.ascontiguousarray(np.asarray(inp["positions"][b], np.int32).reshape(1, SEQ)),
        norm_g=_col(inp["norm_g"][l], 8),
        w_ada=np.ascontiguousarray(f(inp["w_ada"][l])[:, 0:2048]),
        b_ada=_col(f(inp["b_ada"][l])[0:2048], 16),
        w_in=w_in_c,
        qng=_col(inp["mla_q_norm_g"][l], 2),
        w_uq=wuq_c,
        kvng=_col(inp["mla_kv_norm_g"][l], 1),
        w_ukv=wukv_c,
        gvec=gvec,
        rb=np.ascontiguousarray(f(inp["rel_bias"])[:, mh]),
    )
    d.update(consts)
    return d


def kernel(**inputs):
    inp = {k: np.asarray(v) for k, v in inputs.items()}
    consts, sgn, invf = _constants()
    x = np.asarray(inp["x"], np.float32)
    for l in range(2):
        ncA = _prog("A")
        maps = []
        for core in range(8):
            b, hg = core // 4, core % 4
            maps.append(_layer_inputs_A(l, b, hg, x[b], inp, consts, sgn, invf))
        res = run_bass_kernel_spmd(ncA, maps, core_ids=list(range(8)))
        mixT = np.zeros((NB, D, SEQ), ml_dtypes.bfloat16)
        for core in range(8):
            b, hg = core // 4, core % 4
            m = np.asarray(res.results[core]["mixT"])
            for e in range(2):
                mixT[b, (2 * hg + e) * 64:(2 * hg + e + 1) * 64] = m[e * 64:(e + 1) * 64]
                mixT[b, 512 + (2 * hg + e) * 64:512 + (2 * hg + e + 1) * 64] = m[128 + e * 64:128 + (e + 1) * 64]
        ncB = _prog("B")
        maps = []
        xf = x.reshape(NB * SEQ, D)
        for core in range(8):
            b, j = core // 4, core % 4
            maps.append(dict(
                x=np.ascontiguousarray(xf[core * TB:(core + 1) * TB]),
                mixT=np.ascontiguousarray(mixT[b][:, j * TB:(j + 1) * TB]),
                w_out=np.ascontiguousarray(np.asarray(inp["w_out"][l], np.float32)),
                c=_col(inp["c"][b], 8),
                w_ada_g=np.ascontiguousarray(np.asarray(inp["w_ada"][l], np.float32)[:, 2048:3072]),
                b_ada_g=np.ascontiguousarray(np.asarray(inp["b_ada"][l], np.float32)[2048:3072].reshape(1, D)),
            ))
        res = run_bass_kernel_spmd(ncB, maps, core_ids=list(range(8)))
        x = np.concatenate([np.asarray(res.results[core]["xo"]) for core in range(8)], axis=0).reshape(NB, SEQ, D)
        x = np.ascontiguousarray(x, dtype=np.float32)
    return x
```
